# Optimizing a Trainium2 kernel written in Bass

```python
import jax, jax.numpy as jnp
from jax import lax
import numpy as np

D_MODEL = 1024
BATCH = 8
SEQ = 4096
DEPTH = 4

GRID_W = 64
CTX_LEN = 256
N_MOD = 6
NORM_EPS = 1e-6
ATT_HEADS = 8
ATT_KV_HEADS = 2
HEAD_DIM = 64
WINDOW = 128
ATT_BLOCK = 128
ROPE_BASE = 10000.0
ROPE_PAIRS = HEAD_DIM // 4
ATT_Q = ATT_HEADS * HEAD_DIM
ATT_KV = ATT_KV_HEADS * HEAD_DIM
MLSTM_HEADS = 4
MLSTM_HEAD_DIM = 64
MLSTM_CHUNK = 64
M_W = MLSTM_HEADS * MLSTM_HEAD_DIM
M_GATES = 4 * MLSTM_HEADS
LRU_WIDTH = 256
LRU_BLOCKS = 4
LRU_BW = LRU_WIDTH // LRU_BLOCKS
LRU_C = 8.0
CONV_WIDTH = 4
CONV_LEFT = CONV_WIDTH // 2
D_IN = ATT_Q + 2 * ATT_KV + 4 * M_W + M_GATES + 2 * LRU_WIDTH
D_MIX = ATT_Q + M_W + LRU_WIDTH
D_FF = -(-8 * D_MODEL // (3 * 256)) * 256

kernel_name = "hybrid_prefix_dit_mlstm_rglru_swa"


def rms_norm(x, g):
    xf = x.astype(jnp.float32)
    y = xf * lax.rsqrt(jnp.mean(xf * xf, axis=-1, keepdims=True) + NORM_EPS)
    return (y * g.astype(jnp.float32)).astype(x.dtype)


def modulate(h, shift, scale):
    return h * (1 + scale) + shift


def flip_streams(a, n_ctx):
    return jnp.concatenate([a[:, :n_ctx][:, ::-1], a[:, n_ctx:][:, ::-1]], axis=1)


def rotate(x, cos, sin):
    x1, x2 = jnp.split(x, 2, axis=-1)
    cos, sin = cos.astype(x.dtype), sin.astype(x.dtype)
    return jnp.concatenate([x1 * cos - x2 * sin, x2 * cos + x1 * sin], axis=-1)


def rope_axial(x, rope):
    cos_r, sin_r, cos_c, sin_c = rope
    half = HEAD_DIM // 2
    return jnp.concatenate([rotate(x[..., :half], cos_r, sin_r), rotate(x[..., half:], cos_c, sin_c)], axis=-1)


def windowed_gqa(q_l, k_l, v_l, q_c, k_c, v_c, sink, rope, need_ctx):
    B, S, _ = q_l.shape
    n_ctx = k_c.shape[1]
    G = ATT_HEADS // ATT_KV_HEADS
    scale = HEAD_DIM ** -0.5
    f32 = jnp.float32
    q = rope_axial(q_l.reshape(B, S, ATT_HEADS, HEAD_DIM), rope) * scale
    k = rope_axial(k_l.reshape(B, S, ATT_KV_HEADS, HEAD_DIM), rope)
    v = v_l.reshape(B, S, ATT_KV_HEADS, HEAD_DIM)
    kc = k_c.reshape(B, n_ctx, ATT_KV_HEADS, HEAD_DIM)
    vc = v_c.reshape(B, n_ctx, ATT_KV_HEADS, HEAD_DIM)
    nb = S // ATT_BLOCK
    qb = q.reshape(B, nb, ATT_BLOCK, ATT_KV_HEADS, G, HEAD_DIM)

    def band(a):
        ap = jnp.pad(a, ((0, 0), (ATT_BLOCK, ATT_BLOCK), (0, 0), (0, 0)))
        ap = ap.reshape(B, nb + 2, ATT_BLOCK, ATT_KV_HEADS, HEAD_DIM)
        return jnp.concatenate([ap[:, :-2], ap[:, 1:-1], ap[:, 2:]], axis=2)

    kw, vw = band(k), band(v)
    blk = jnp.arange(nb)[:, None, None] * ATT_BLOCK
    qpos = blk + jnp.arange(ATT_BLOCK)[None, :, None]
    kpos = blk - ATT_BLOCK + jnp.arange(3 * ATT_BLOCK)[None, None, :]
    valid = (jnp.abs(qpos - kpos) <= WINDOW) & (kpos >= 0) & (kpos < S)
    s_loc = jnp.einsum('bnqkgd,bnskd->bnkgqs', qb, kw).astype(f32)
    s_loc = jnp.where(valid[None, :, None, None], s_loc, -jnp.inf)
    s_ctx = jnp.einsum('bnqkgd,bckd->bnkgqc', qb, kc).astype(f32)
    sink_h = sink.astype(f32).reshape(ATT_KV_HEADS, G, 1, 1)
    s_sink = jnp.broadcast_to(sink_h, s_loc.shape[:-1] + (1,))
    p = jax.nn.softmax(jnp.concatenate([s_loc, s_ctx, s_sink], axis=-1), axis=-1).astype(v.dtype)
    L = 3 * ATT_BLOCK
    o = (jnp.einsum('bnkgqs,bnskd->bnqkgd', p[..., :L], vw)
         + jnp.einsum('bnkgqc,bckd->bnqkgd', p[..., L:L + n_ctx], vc))
    out_l = o.reshape(B, S, ATT_Q)
    out_c = None
    if need_ctx:
        qc = q_c.reshape(B, n_ctx, ATT_KV_HEADS, G, HEAD_DIM) * scale
        s = jnp.einsum('bqkgd,bckd->bkgqc', qc, kc).astype(f32)
        s = jnp.concatenate([s, jnp.broadcast_to(sink_h, s.shape[:-1] + (1,))], axis=-1)
        pc = jax.nn.softmax(s, axis=-1).astype(vc.dtype)
        out_c = jnp.einsum('bkgqc,bckd->bqkgd', pc[..., :n_ctx], vc).reshape(B, n_ctx, ATT_Q)
    return out_l, out_c


def mlstm_chunkwise(q, k, v, log_i, log_f):
    B, T, H, dh = q.shape
    L = MLSTM_CHUNK
    N = T // L
    to_chunks = lambda a: jnp.moveaxis(a.reshape(B, N, L, H, -1), 3, 1)
    qc, kc, vc = to_chunks(q), to_chunks(k), to_chunks(v)
    li = jnp.moveaxis(log_i.reshape(B, N, L, H), 3, 1)
    lf = jnp.moveaxis(log_f.reshape(B, N, L, H), 3, 1)
    b = jnp.cumsum(lf, axis=-1)
    g = b[..., -1]
    a = g[..., None] - b + li
    m_loc = jnp.max(a, axis=-1)
    w = jnp.exp(a - m_loc[..., None])
    C_loc = jnp.einsum('bhnl,bhnld,bhnle->bhnde', w, kc, vc)
    n_loc = jnp.einsum('bhnl,bhnld->bhnd', w, kc)

    def step(carry, xs):
        C, n, m = carry
        g_j, m_j, C_j, n_j = xs
        m_new = jnp.maximum(g_j + m, m_j)
        f_prev = jnp.exp(g_j + m - m_new)
        f_loc = jnp.exp(m_j - m_new)
        C_new = f_prev[..., None, None] * C + f_loc[..., None, None] * C_j
        n_new = f_prev[..., None] * n + f_loc[..., None] * n_j
        return (C_new, n_new, m_new), (C, n, m)

    init = (jnp.zeros((B, H, dh, dh), q.dtype), jnp.zeros((B, H, dh), q.dtype), jnp.zeros((B, H), q.dtype))
    xs = (jnp.moveaxis(g, 2, 0), jnp.moveaxis(m_loc, 2, 0), jnp.moveaxis(C_loc, 2, 0), jnp.moveaxis(n_loc, 2, 0))
    _, (C0, n0, m0) = lax.scan(step, init, xs)
    C0, n0, m0 = jnp.moveaxis(C0, 0, 2), jnp.moveaxis(n0, 0, 2), jnp.moveaxis(m0, 0, 2)
    tril = jnp.tril(jnp.ones((L, L), dtype=bool))
    d_log = jnp.where(tril, b[..., :, None] - b[..., None, :] + li[..., None, :], -jnp.inf)
    m_inter = b + m0[..., None]
    m_t = jnp.maximum(jnp.max(d_log, axis=-1), m_inter)
    f_inter = jnp.exp(m_inter - m_t)
    s = jnp.einsum('bhntd,bhnsd->bhnts', qc, kc) * jnp.exp(d_log - m_t[..., None])
    num = (f_inter[..., None] * jnp.einsum('bhntd,bhnde->bhnte', qc, C0)
           + jnp.einsum('bhnts,bhnse->bhnte', s, vc))
    den = f_inter * jnp.einsum('bhntd,bhnd->bhnt', qc, n0) + jnp.sum(s, axis=-1)
    h = num / jnp.maximum(jnp.abs(den), jnp.exp(-m_t))[..., None]
    return jnp.moveaxis(h, 1, 3).reshape(B, T, H, dh)


def head_norm(h, g):
    mu = jnp.mean(h, axis=-1, keepdims=True)
    var = jnp.mean(jnp.square(h - mu), axis=-1, keepdims=True)
    y = (h - mu) * lax.rsqrt(var + NORM_EPS)
    return y.reshape(h.shape[0], h.shape[1], -1) * g.astype(jnp.float32)


def mlstm_mixer(q_l, k_l, v_l, o_l, g_l, q_c, k_c, v_c, o_c, g_c, gate_b, norm_g, need_ctx):
    n_ctx = q_c.shape[1]
    f32 = jnp.float32
    heads = lambda a: a.astype(f32).reshape(a.shape[0], a.shape[1], MLSTM_HEADS, MLSTM_HEAD_DIM)
    gates = lambda a: a.astype(f32).reshape(a.shape[0], a.shape[1], 4, MLSTM_HEADS) + gate_b.astype(f32)
    q = jnp.concatenate([heads(q_c), heads(q_l)], axis=1)
    k = jnp.concatenate([heads(k_c), heads(k_l)], axis=1) * (MLSTM_HEAD_DIM ** -0.5)
    v = jnp.concatenate([heads(v_c), heads(v_l)], axis=1)
    gt = jnp.concatenate([gates(g_c), gates(g_l)], axis=1)
    h_f = mlstm_chunkwise(q, k, v, gt[:, :, 0], jax.nn.log_sigmoid(gt[:, :, 1]))
    fl = lambda a: flip_streams(a, n_ctx)
    h_b = fl(mlstm_chunkwise(fl(q), fl(k), fl(v), fl(gt[:, :, 2]), fl(jax.nn.log_sigmoid(gt[:, :, 3]))))
    h = h_f + h_b
    finish = lambda hh, o: (jax.nn.sigmoid(o.astype(f32)) * head_norm(hh, norm_g)).astype(o.dtype)
    out_l = finish(h[:, n_ctx:], o_l)
    out_c = finish(h[:, :n_ctx], o_c) if need_ctx else None
    return out_l, out_c


def dwconv(x, w, b):
    T = x.shape[1]
    xp = jnp.pad(x, ((0, 0), (CONV_LEFT, CONV_WIDTH - 1 - CONV_LEFT), (0, 0)))
    return b + sum(w[j] * xp[:, j:j + T] for j in range(CONV_WIDTH))


def block_diag(x, w):
    B, T, _ = x.shape
    y = jnp.einsum('btni,nij->btnj', x.reshape(B, T, LRU_BLOCKS, LRU_BW), w)
    return y.reshape(B, T, LRU_WIDTH)


def linear_scan(a, u):
    def comb(lhs, rhs):
        al, ul = lhs
        ar, ur = rhs
        return al * ar, ar * ul + ur
    _, h = lax.associative_scan(comb, (a, u), axis=1)
    return h


def rglru_mixer(x_l, y_l, x_c, y_c, conv_w, conv_b, gate_w, gate_b, lam, need_ctx):
    n_ctx = x_c.shape[1]
    f32 = jnp.float32
    cw, cb = conv_w.astype(f32), conv_b.astype(f32)
    seq = jnp.concatenate([dwconv(x_c.astype(f32), cw, cb), dwconv(x_l.astype(f32), cw, cb)], axis=1)
    h = 0.0
    for d in range(2):
        r = jax.nn.sigmoid(block_diag(seq, gate_w[d, 0].astype(f32)) + gate_b[d, 0].astype(f32))
        i = jax.nn.sigmoid(block_diag(seq, gate_w[d, 1].astype(f32)) + gate_b[d, 1].astype(f32))
        log_a = -LRU_C * r * jax.nn.softplus(-lam[d].astype(f32))
        a = jnp.exp(log_a)
        u = jnp.sqrt(-jnp.expm1(2.0 * log_a)) * (i * seq)
        if d == 0:
            h = h + linear_scan(a, u)
        else:
            h = h + flip_streams(linear_scan(flip_streams(a, n_ctx), flip_streams(u, n_ctx)), n_ctx)
    out_l = (h[:, n_ctx:] * jax.nn.gelu(y_l.astype(f32))).astype(x_l.dtype)
    out_c = (h[:, :n_ctx] * jax.nn.gelu(y_c.astype(f32))).astype(x_c.dtype) if need_ctx else None
    return out_l, out_c


def token_mixers(p_l, p_c, rope, sink, m_gate_b, m_norm, conv_w, conv_b, lru_w, lru_b, lru_lam, need_ctx):
    sizes = [ATT_Q, ATT_KV, ATT_KV, M_W, M_W, M_W, M_W, M_GATES, LRU_WIDTH, LRU_WIDTH]
    pts = np.cumsum(sizes)[:-1].tolist()
    aq, ak, av, mq, mk, mv, mo, mg, rx, ry = jnp.split(p_l, pts, axis=-1)
    cq, ck, cv, cmq, cmk, cmv, cmo, cmg, crx, cry = jnp.split(p_c, pts, axis=-1)
    att_l, att_c = windowed_gqa(aq, ak, av, cq, ck, cv, sink, rope, need_ctx)
    mem_l, mem_c = mlstm_mixer(mq, mk, mv, mo, mg, cmq, cmk, cmv, cmo, cmg, m_gate_b, m_norm, need_ctx)
    rec_l, rec_c = rglru_mixer(rx, ry, crx, cry, conv_w, conv_b, lru_w, lru_b, lru_lam, need_ctx)
    out_l = jnp.concatenate([att_l, mem_l, rec_l], axis=-1)
    out_c = jnp.concatenate([att_c, mem_c, rec_c], axis=-1) if need_ctx else None
    return out_l, out_c


def swiglu(h, w_in, w_out):
    gate, up = jnp.split(h @ w_in, 2, axis=-1)
    return (jax.nn.silu(gate) * up) @ w_out


def setup_inputs(seed: int = 0) -> dict:
    key = jax.random.key(seed)
    ks = jax.random.split(key, 20)
    f32 = jnp.float32
    nrm = lambda k, shape, s: jax.random.normal(k, shape, f32) * s
    x = nrm(ks[0], (BATCH, SEQ, D_MODEL), 1.0)
    c = nrm(ks[1], (BATCH, D_MODEL), 1.0)
    ctx = nrm(ks[2], (BATCH, CTX_LEN, D_MODEL), 1.0)
    c_ctx = nrm(ks[3], (D_MODEL,), 1.0)
    w_ada = nrm(ks[4], (DEPTH, D_MODEL, N_MOD * D_MODEL), 0.5 * D_MODEL ** -0.5)
    b_ada = nrm(ks[5], (DEPTH, N_MOD * D_MODEL), 0.01)
    norm_gain = 1.0 + nrm(ks[6], (DEPTH, 4, D_MODEL), 0.05)
    w_in = nrm(ks[7], (DEPTH, D_MODEL, D_IN), D_MODEL ** -0.5)
    w_out = nrm(ks[8], (DEPTH, D_MIX, D_MODEL), D_MIX ** -0.5)
    attn_sink = nrm(ks[9], (DEPTH, ATT_HEADS), 0.5)
    i_bias = nrm(ks[10], (DEPTH, 2, MLSTM_HEADS), 0.1)
    f_bias = 3.0 + 3.0 * jax.random.uniform(ks[11], (DEPTH, 2, MLSTM_HEADS), f32)
    mlstm_gate_b = jnp.stack([i_bias[:, 0], f_bias[:, 0], i_bias[:, 1], f_bias[:, 1]], axis=1)
    mlstm_norm = 1.0 + nrm(ks[12], (DEPTH, M_W), 0.05)
    conv_w = nrm(ks[13], (DEPTH, CONV_WIDTH, LRU_WIDTH), CONV_WIDTH ** -0.5)
    conv_b = nrm(ks[14], (DEPTH, LRU_WIDTH), 0.01)
    lru_gate_w = nrm(ks[15], (DEPTH, 2, 2, LRU_BLOCKS, LRU_BW, LRU_BW), LRU_BW ** -0.5)
    lru_gate_b = nrm(ks[16], (DEPTH, 2, 2, LRU_WIDTH), 0.01)
    a_c = jax.random.uniform(ks[17], (DEPTH, 2, LRU_WIDTH), f32, 0.9, 0.999)
    p = a_c ** (1.0 / LRU_C)
    lru_lam = jnp.log(p) - jnp.log1p(-p)
    w_ffn_in = nrm(ks[18], (DEPTH, D_MODEL, 2 * D_FF), D_MODEL ** -0.5)
    w_ffn_out = nrm(ks[19], (DEPTH, D_FF, D_MODEL), D_FF ** -0.5)
    return {"x": x, "c": c, "ctx": ctx, "c_ctx": c_ctx, "w_ada": w_ada, "b_ada": b_ada,
            "norm_gain": norm_gain, "w_in": w_in, "w_out": w_out, "attn_sink": attn_sink,
            "mlstm_gate_b": mlstm_gate_b, "mlstm_norm": mlstm_norm, "conv_w": conv_w, "conv_b": conv_b,
            "lru_gate_w": lru_gate_w, "lru_gate_b": lru_gate_b, "lru_lam": lru_lam,
            "w_ffn_in": w_ffn_in, "w_ffn_out": w_ffn_out}


def reference(x, c, ctx, c_ctx, w_ada, b_ada, norm_gain, w_in, w_out, attn_sink, mlstm_gate_b, mlstm_norm,
              conv_w, conv_b, lru_gate_w, lru_gate_b, lru_lam, w_ffn_in, w_ffn_out):
    n_lat = x.shape[1]
    n_rows = n_lat // GRID_W
    t = jnp.arange(n_rows * GRID_W)
    row = (t // GRID_W).astype(jnp.float32)
    col = (t % GRID_W).astype(jnp.float32)
    freqs = ROPE_BASE ** (-jnp.arange(ROPE_PAIRS, dtype=jnp.float32) / ROPE_PAIRS)
    ang_r = (row[:, None] * freqs)[:, None, :]
    ang_c = (col[:, None] * freqs)[:, None, :]
    rope = (jnp.cos(ang_r), jnp.sin(ang_r), jnp.cos(ang_c), jnp.sin(ang_c))
    s_lat = jax.nn.silu(c)
    s_ctx = jax.nn.silu(c_ctx)
    for l in range(DEPTH):
        need_ctx = l < DEPTH - 1
        mod_l = jnp.split((s_lat @ w_ada[l] + b_ada[l])[:, None, :], N_MOD, axis=-1)
        mod_c = jnp.split(s_ctx @ w_ada[l] + b_ada[l], N_MOD, axis=-1)
        h_l = modulate(rms_norm(x, norm_gain[l, 0]), mod_l[0], mod_l[1])
        h_c = modulate(rms_norm(ctx, norm_gain[l, 0]), mod_c[0], mod_c[1])
        mix_l, mix_c = token_mixers(h_l @ w_in[l], h_c @ w_in[l], rope, attn_sink[l], mlstm_gate_b[l], mlstm_norm[l],
                                    conv_w[l], conv_b[l], lru_gate_w[l], lru_gate_b[l], lru_lam[l], need_ctx)
        x = x + mod_l[2] * rms_norm(mix_l @ w_out[l], norm_gain[l, 1])
        f_l = swiglu(modulate(rms_norm(x, norm_gain[l, 2]), mod_l[3], mod_l[4]), w_ffn_in[l], w_ffn_out[l])
        x = x + mod_l[5] * rms_norm(f_l, norm_gain[l, 3])
        if need_ctx:
            ctx = ctx + mod_c[2] * rms_norm(mix_c @ w_out[l], norm_gain[l, 1])
            f_c = swiglu(modulate(rms_norm(ctx, norm_gain[l, 2]), mod_c[3], mod_c[4]), w_ffn_in[l], w_ffn_out[l])
            ctx = ctx + mod_c[5] * rms_norm(f_c, norm_gain[l, 3])
    return x
```

```python
import numpy as np
from contextlib import ExitStack
import concourse.bass as bass
import concourse.mybir as mybir
from concourse.bass_utils import run_bass_kernel_spmd

F32 = mybir.dt.float32
BF16 = mybir.dt.bfloat16
AF = mybir.ActivationFunctionType
ALU = mybir.AluOpType
AX = mybir.AxisListType

D = 1024
NCTX = 256
DIN = 2320
DFF = 2816
EPS = 1e-6
ENGS = ("pe", "act", "dve", "pool", "sp")
import os
POOL_TO = os.environ.get("POOL_TO", None)
P1S = int(os.environ.get("P1S", "99"))
P1T = int(os.environ.get("P1T", "99"))
MLS = int(os.environ.get("MLS", "99"))
MLT = int(os.environ.get("MLT", "99"))


class Buf:
    __slots__ = ("name", "w", "r", "sem", "cnt")

    def __init__(self, name):
        self.name = name
        self.w = None
        self.r = {}
        self.sem = None
        self.cnt = 0


class _Rec:
    def __init__(self):
        self.call = None

    def __getattr__(self, name):
        def f(*a, **k):
            self.call = (name, a, k)
            return self
        return f


class Prog:
    def __init__(self, nc, stack):
        self.nc = nc
        self.stack = stack
        self.streams = {k: [] for k in ENGS}
        self.sems = {}
        self.n = {k: 0 for k in ENGS}
        self.known = {k: {} for k in ENGS}
        self.nsem = 0
        self.bufs = []
        self.bufmap = {}
        self.semcnt = {}
        for k in ("pe", "act", "dve", "pool"):
            self.sems[k] = stack.enter_context(nc.semaphore("sem_" + k))
        self.nins = 0

    ST_NAMES = {"xT", "qT", "kT", "vab", "mqT", "mkT", "mk", "mva", "mo", "rx", "ry", "mix", "h2T", "out"}
    W_NAMES = {"win", "wo", "wfi", "wfo", "pm", "bdt", "wa0", "wa1"}

    def buf(self, name):
        if name in self.bufmap:
            return self.bufmap[name]
        b = Buf(name)
        key = "g%d_%s" % (len(self.bufmap), name)
        b.sem = key
        self.bufmap[name] = b
        self.bufs.append(b)
        return b

    def _newsem(self, name):
        key = "d%d_%s" % (self.nsem, name)
        self.nsem += 1
        self.sems[key] = self.stack.enter_context(self.nc.semaphore(key))
        return key

    def _waits(self, eng, R, W):
        waits = {}
        kn = self.known[eng]

        def need(dep, war=False):
            if dep is None:
                return
            key, val = dep
            if key == eng and (eng == "pe" or war):
                return
            if kn.get(key, 0) >= val:
                return
            if waits.get(key, 0) < val:
                waits[key] = val
        for b in R:
            need(b.w)
        for b in W:
            need(b.w)
            for k, v in b.r.items():
                need((k, v), war=True)
        st = self.streams[eng]
        for key, val in waits.items():
            kn[key] = val
            sem = self.sems[key]
            st.append(lambda e, sem=sem, val=val: e.wait_ge(sem, val))

    def op(self, eng, fn, R=(), W=(), signal=True):
        if eng == "pool" and POOL_TO is not None:
            eng = POOL_TO
        assert signal or eng == "pe"
        self._waits(eng, R, W)
        if signal:
            self.n[eng] += 1
            my = (eng, self.n[eng])
        else:
            my = (eng, self.n[eng] + 1)
        sem = self.sems[eng]
        rec = _Rec()
        fn(rec)
        name, a, k = rec.call
        if signal:
            self.streams[eng].append(lambda e, name=name, a=a, k=k, sem=sem: getattr(e, name)(*a, **k).then_inc(sem, 1))
        else:
            self.streams[eng].append(lambda e, name=name, a=a, k=k: getattr(e, name)(*a, **k))
        self.nins += 1
        for b in R:
            if b.r.get(my[0], 0) < my[1]:
                b.r[my[0]] = my[1]
        for b in W:
            b.w = my
            b.r = {}

    def dma(self, q, out_ap, in_ap, dst, R=(), W=None, **kw):
        if W is None:
            W = (dst,)
        self._waits(q, R, W)
        if dst.sem not in self.sems:
            self.sems[dst.sem] = self.stack.enter_context(self.nc.semaphore(dst.sem))
            self.semcnt[dst.sem] = 0
            self.nsem += 1
        self.semcnt[dst.sem] += 16
        my = (dst.sem, self.semcnt[dst.sem])
        sem = self.sems[dst.sem]
        self.streams[q].append(
            lambda e, o=out_ap, i=in_ap, sem=sem, kw=kw: e.dma_start(out=o, in_=i, **kw).then_inc(sem, 16))
        self.nins += 1
        for b in R:
            if b.r.get(my[0], 0) < my[1]:
                b.r[my[0]] = my[1]
        for b in W:
            b.w = my
            b.r = {}

    def barrier(self):
        allb = [b for b in self.bufs]
        for eng in ENGS:
            self._waits(eng, allb, allb)
            kn = self.known[eng]
            for k in ("pe", "act", "dve", "pool"):
                if k != eng and kn.get(k, 0) < self.n[k] and self.n[k] > 0:
                    kn[k] = self.n[k]
                    sem = self.sems[k]
                    self.streams[eng].append(lambda e, sem=sem, val=self.n[k]: e.wait_ge(sem, val))

    def emit(self):
        with self.nc.Block() as block:
            @block.sync
            def _(e):
                for f in self.streams["sp"]:
                    f(e)

            @block.gpsimd
            def _(e):
                for f in self.streams["pool"]:
                    f(e)

            @block.tensor
            def _(e):
                for f in self.streams["pe"]:
                    f(e)

            @block.vector
            def _(e):
                for f in self.streams["dve"]:
                    f(e)

            @block.scalar
            def _(e):
                for f in self.streams["act"]:
                    f(e)


class Arena:
    def __init__(self, t, nwords):
        self.t = t
        self.n = nwords
        self.off = 0

    def alloc(self, shape, dt=F32):
        p = shape[0]
        free = list(shape[1:])
        n = int(np.prod(free))
        words = n if dt == F32 else (n + 1) // 2
        assert self.off + words <= self.n, "arena overflow %d + %d > %d" % (self.off, words, self.n)
        v = self.t[:, self.off:self.off + words]
        self.off += words
        if dt != F32:
            v = v.bitcast(dt)[:, 0:n]
        if len(free) == 2:
            v = v.rearrange("p (a b) -> p a b", b=free[1])
        elif len(free) == 3:
            v = v.rearrange("p (a b c) -> p a b c", b=free[1], c=free[2])
        elif len(free) == 4:
            v = v.rearrange("p (a b c d) -> p a b c d", b=free[1], c=free[2], d=free[3])
        if p < 128:
            v = v[0:p]
        return v


def build(L, TL, dbg=False, ctx_out=False, phases=("ada", "p1", "att", "lru", "mls", "p3a", "p3b")):
    T = NCTX + TL
    NT = T // 128
    NLT = TL // 128
    nc = bass.Bass("TRN2", target_bir_lowering=False)
    skind = "ExternalOutput" if dbg else "Internal"

    def din(name, shape, dt=F32):
        return nc.dram_tensor(name, list(shape), dt, kind="ExternalInput").ap()

    def dscr(name, shape, dt=F32):
        return nc.dram_tensor(name, list(shape), dt, kind=skind).ap()

    x_d = din("x", [TL, D])
    ctx_d = din("ctx", [NCTX, D])
    cT_d = din("cT", [128, 8, 2])
    wada_d = din("w_ada", [L, D, 6 * D])
    bada_d = din("b_ada", [L, 6 * D])
    ng_d = din("ng", [L, 128, 4, 8])
    win_d = din("w_in", [L, D, DIN])
    wout_d = din("w_out", [L, D, D])
    wfi_d = din("w_ffn_in", [L, D, 2 * DFF])
    wfo_d = din("w_ffn_out", [L, DFF, D])
    sink_d = din("sink_rep", [L, 2, 512])
    mbi_d = din("mbi", [L, 8])
    mbf_d = din("mbf", [L, 8])
    mnorm_d = din("mnorm", [L, 256])
    cw_d = din("cw", [L, 128, 2, 4])
    cb_d = din("cb", [L, 128, 2])
    bd_d = din("bd", [L, 8, 128, 128])
    gb_d = din("gb", [L, 128, 8])
    lam_d = din("lam", [L, 128, 4])
    identf_d = din("ident", [128, 128])
    pm_d = din("pm", [128, 128])
    cos_d = din("cosT", [128, T])
    sin_d = din("sinT", [128, T])
    amask_d = din("amask", [2, 128, 128])
    mmask_d = din("mmask", [2, 128, 128])
    out_d = nc.dram_tensor("out", [TL, D], F32, kind="ExternalOutput").ap()
    if ctx_out:
        ctxo_d = nc.dram_tensor("ctx_out", [NCTX, D], F32, kind="ExternalOutput").ap()

    xT_d = dscr("xT_s", [8, 128, T])
    qT_d = dscr("qT_s", [128, 4, T], BF16)
    kT_d = dscr("kT_s", [128, T], BF16)
    vab_d = dscr("vab_s", [T, 512], BF16)
    mqT_d = dscr("mqT_s", [2, 128, T], BF16)
    mkT_d = dscr("mkT_s", [2, 128, T], BF16)
    mk_d = dscr("mk_s", [T, 256], BF16)
    mva_d = dscr("mva_s", [T, 264], BF16)
    mo_d = dscr("mo_s", [T, 256])
    rxT_d = dscr("rxT_s", [2, 128, T])
    ryT_d = dscr("ryT_s", [2, 128, T])
    mixT_d = dscr("mixT_s", [8, 128, T], BF16)
    h2T_d = dscr("h2T_s", [8, 128, T], BF16)

    with ExitStack() as st:
        P = Prog(nc, st)
        NW = 51000
        arena_t = st.enter_context(nc.sbuf_tensor("arena", [128, NW], F32))
        pbt = [st.enter_context(nc.psum_tensor("pb%d" % i, [128, 512], F32)) for i in range(8)]
        pb = [t[:] for t in pbt]
        bpb = [P.buf("pb%d" % i) for i in range(8)]
        A = Arena(arena_t, NW)

        b_xT = P.buf("xT")
        b_q = P.buf("qT"); b_k = P.buf("kT"); b_vab = P.buf("vab"); b_mq = P.buf("mqT"); b_mkT = P.buf("mkT")
        b_mk = P.buf("mk"); b_mva = P.buf("mva"); b_mo = P.buf("mo")
        b_rx = P.buf("rx"); b_ry = P.buf("ry"); b_mix = P.buf("mix"); b_h2 = P.buf("h2T"); b_out = P.buf("out")

        identf = A.alloc([128, 128]); b_identf = P.buf("identf")
        identb = A.alloc([128, 128], BF16); b_identb = P.buf("identb")
        onesb = A.alloc([128, 128], BF16); b_onesb = P.buf("onesb")
        S2 = A.alloc([128, 8, 2]); b_S2 = P.buf("S2")
        modT = A.alloc([128, 6, 8, 2]); b_modT = P.buf("modT")
        ngt = A.alloc([128, 4, 8]); b_ngt = P.buf("ngt")
        gm1 = A.alloc([128, 8, 2]); gm2 = A.alloc([128, 8, 2]); gg1 = A.alloc([128, 8, 2]); gg2 = A.alloc([128, 8, 2])
        b_gm = P.buf("gm")
        epsc = A.alloc([128, 1]); b_epsc = P.buf("epsc")
        gall = A.alloc([128, NT, 16]); b_gall = P.buf("gall")
        PERS = A.off

        P.dma("sp", identf, identf_d, b_identf)
        P.dma("pool", identb, identf_d, b_identb)
        P.op("dve", lambda e: e.memset(onesb, 1.0), W=[b_onesb])
        P.op("dve", lambda e: e.memset(epsc, EPS), W=[b_epsc])
        P.dma("sp", S2, cT_d, b_S2)
        P.op("act", lambda e: e.activation(out=S2, in_=S2, func=AF.Silu), R=[b_S2], W=[b_S2])

        def mmg(out_ap, pairs, R, W):
            n = len(pairs)
            for i, (l, r) in enumerate(pairs):
                P.op("pe", lambda e, l=l, r=r, i=i: e.matmul(out_ap, l, r, start=(i == 0), stop=(i == n - 1)),
                     R=R, W=W, signal=(i == n - 1))

        def blocks(nbmax):
            bl = [(t0, min(nbmax, 256), 1) for t0 in range(0, NCTX, min(nbmax, 256))]
            bl += [(NCTX + i * nbmax, nbmax, 0) for i in range(TL // nbmax)]
            return bl

        def rms_rstd(src_sq, nb, ssb, b_sq, b_rstd, rstd, sd):
            mmg(pb[ssb][:, 0:nb], [(onesb, src_sq[:, kc, :]) for kc in range(8)], R=[b_sq, b_onesb], W=[bpb[ssb]])
            P.op("act", lambda e: e.activation(out=sd, in_=pb[ssb][:, 0:nb], func=AF.Sqrt, bias=epsc[:, 0:1], scale=1.0 / D),
                 R=[bpb[ssb], b_epsc], W=[b_rstd])
            P.op("dve", lambda e: e.reciprocal(out=rstd, in_=sd), R=[b_rstd], W=[b_rstd])

        A.off = PERS
        xin = [A.alloc([128, D]) for _ in range(2)]; b_xin = [P.buf("xin%d" % i) for i in range(2)]
        xtr = [A.alloc([128, 8, 128]) for _ in range(2)]; b_xtr = [P.buf("xtr%d" % i) for i in range(2)]
        for tt in range(NT):
            i2 = tt % 2
            src = ctx_d[tt * 128:(tt + 1) * 128, :] if tt < 2 else x_d[(tt - 2) * 128:(tt - 1) * 128, :]
            P.dma("sp", xin[i2], src, b_xin[i2])
            for half in range(2):
                bk = (tt * 2 + half) % 4
                for c4 in range(4):
                    kc = half * 4 + c4
                    P.op("pe", lambda e, kc=kc, c4=c4, bk=bk, i2=i2: e.transpose(
                        out=pb[bk][:, c4 * 128:(c4 + 1) * 128], in_=xin[i2][:, kc * 128:(kc + 1) * 128], identity=identf),
                        R=[b_xin[i2], b_identf], W=[bpb[bk]], signal=(c4 == 3))
                eng = "act" if half == 0 else "dve"
                if eng == "act":
                    P.op("act", lambda e, bk=bk, i2=i2, half=half: e.activation(
                        out=xtr[i2][:, half * 4:(half + 1) * 4, :], in_=pb[bk].rearrange("p (a b) -> p a b", b=128), func=AF.Copy),
                        R=[bpb[bk]], W=[b_xtr[i2]])
                else:
                    P.op("dve", lambda e, bk=bk, i2=i2, half=half: e.tensor_copy(
                        out=xtr[i2][:, half * 4:(half + 1) * 4, :], in_=pb[bk].rearrange("p (a b) -> p a b", b=128)),
                        R=[bpb[bk]], W=[b_xtr[i2]])
            P.dma("sp", xT_d[:, :, tt * 128:(tt + 1) * 128].rearrange("k p t -> p k t"), xtr[i2], b_xT, R=[b_xtr[i2]])

        def mlstm_phase(l):
            NG = NT * 8
            mm_ = A.alloc([128, 2, 128]); b_mm = P.buf("mmask")
            onesf = A.alloc([128, 128]); b_onesf = P.buf("onesf")
            bib = A.alloc([128, 8]); bfb = A.alloc([128, 8]); b_gb = P.buf("gbias")
            mnb = A.alloc([128, 256]); b_mnb = P.buf("mnb")
            li = A.alloc([128, NT, 8]); lf = A.alloc([128, NT, 8]); cc = A.alloc([128, NT, 8]); gbc = A.alloc([128, NT, 8])
            tmp = A.alloc([128, NT, 8]); bb = A.alloc([128, NT, 8]); G2 = A.alloc([128, NT, 8])
            dec = A.alloc([128, NT, 8]); egw = A.alloc([128, NT, 8]); ww = A.alloc([128, NT, 8])
            EGs = A.alloc([128, NT, 2, 2]); b_g = P.buf("mgates")
            ktok = A.alloc([128, NT, 256], BF16); b_ktok = P.buf("ktok")
            va = A.alloc([128, NT, 4, 66], BF16); b_va = P.buf("va")
            SB = A.alloc([128, NT, 2, 4, 66], BF16); b_SB = P.buf("SB")
            mqs = A.alloc([128, 2, T], BF16); b_mqs = P.buf("mqs")
            mks = A.alloc([128, 4, T], BF16); b_mks = P.buf("mks")
            Sst = [A.alloc([128, 2, 65]) for _ in range(2)]; b_S = [P.buf("S%d" % i) for i in range(2)]
            kw = [A.alloc([128, 4, 64], BF16) for _ in range(2)]; b_kw = [P.buf("kw%d" % i) for i in range(2)]
            STf = A.alloc([128, 4, 128], BF16); STb = A.alloc([128, 4, 128], BF16); b_ST = P.buf("ST")
            vw = A.alloc([128, 8, 66], BF16); b_vw = P.buf("vw")
            NDs = A.alloc([128, 8, 65]); b_ND = P.buf("NDs")
            dn = A.alloc([128, 8]); b_dn = P.buf("dn")
            Hh = A.alloc([128, 8, 64]); b_Hh = P.buf("Hh")
            Hs = A.alloc([128, 4, 64]); b_Hs = P.buf("Hs")
            st1 = A.alloc([128, 4]); st2 = A.alloc([128, 4]); b_st = P.buf("st")
            Hc = A.alloc([128, 4, 64]); Hq = A.alloc([128, 4, 64]); b_Hc = P.buf("Hc")
            mot = A.alloc([128, 256]); b_mot = P.buf("mot2")
            memb = A.alloc([128, 256], BF16); b_memb = P.buf("memb")
            memT = [A.alloc([128, 2, 128], BF16) for _ in range(2)]; b_memT = [P.buf("memT%d" % i) for i in range(2)]
            isbt = A.alloc([128, 8]); b_isbt = P.buf("isbt")

            P.dma("sp", mm_, mmask_d.rearrange("m p q -> p m q"), b_mm)
            P.op("dve", lambda e: e.memset(onesf, 1.0), W=[b_onesf])
            P.op("dve", lambda e: e.memset(isbt, 0.0), W=[b_isbt])
            P.op("dve", lambda e: e.memset(isbt[:, 4:8], 1.0), R=[b_isbt], W=[b_isbt])
            P.dma("sp", bib, mbi_d[l:l + 1, :].partition_broadcast(128), b_gb)
            P.dma("sp", bfb, mbf_d[l:l + 1, :].partition_broadcast(128), b_gb)
            P.dma("sp", mnb, mnorm_d[l:l + 1, :].partition_broadcast(128), b_mnb)
            P.dma("sp", ktok, mk_d.rearrange("(n p) c -> p n c", p=128), b_ktok, R=[b_mk])
            P.dma("sp", va.rearrange("p n h e -> p n (h e)"), mva_d.rearrange("(n p) c -> p n c", p=128), b_va, R=[b_mva])
            P.dma("sp", mqs, mqT_d.rearrange("c p t -> p c t"), b_mqs, R=[b_mq])
            P.op("dve", lambda e: e.memset(mks, 0.0), W=[b_mks])
            for h in range(4):
                hs = slice((h % 2) * 64, (h % 2) * 64 + 64)
                P.dma("sp", mks[hs, h, :], mkT_d[h // 2, hs, :], b_mks, R=[b_mkT, b_mks] if h == 0 else [b_mkT])
            f2 = lambda a: a.rearrange("p n r -> p (n r)")
            bc8 = lambda a: a.unsqueeze(1).broadcast_to([128, NT, 8])
            P.op("dve", lambda e: e.tensor_tensor(out=li, in0=gall[:, :, 0:8], in1=bc8(bib), op=ALU.add), R=[b_gall, b_gb], W=[b_g])
            P.op("dve", lambda e: e.tensor_tensor(out=lf, in0=gall[:, :, 8:16], in1=bc8(bfb), op=ALU.add), R=[b_gall, b_gb, b_g], W=[b_g])
            P.op("act", lambda e: e.activation(out=lf, in_=lf, func=AF.Sigmoid), R=[b_g], W=[b_g])
            P.op("act", lambda e: e.activation(out=lf, in_=lf, func=AF.Ln), R=[b_g], W=[b_g])
            P.op("pe", lambda e: e.matmul(pb[0][:, 0:NG], mm_[:, 0, :], f2(lf), start=True, stop=True), R=[b_mm, b_g], W=[bpb[0]])
            P.op("pe", lambda e: e.matmul(pb[1][:, 0:NG], onesf, f2(lf), start=True, stop=True), R=[b_onesf, b_g], W=[bpb[1]])
            P.op("act", lambda e: e.activation(out=f2(cc), in_=pb[0][:, 0:NG], func=AF.Copy), R=[bpb[0], b_g], W=[b_g])
            P.op("dve", lambda e: e.tensor_copy(out=f2(gbc), in_=pb[1][:, 0:NG]), R=[bpb[1], b_g], W=[b_g])
            P.op("dve", lambda e: e.tensor_tensor(out=tmp, in0=gbc, in1=cc, op=ALU.subtract), R=[b_g], W=[b_g])
            P.op("dve", lambda e: e.tensor_tensor(out=tmp, in0=tmp, in1=cc, op=ALU.subtract), R=[b_g], W=[b_g])
            P.op("dve", lambda e: e.tensor_tensor(out=tmp, in0=tmp, in1=lf, op=ALU.add), R=[b_g], W=[b_g])
            P.op("dve", lambda e: e.tensor_tensor(out=tmp, in0=tmp, in1=bc8(isbt), op=ALU.mult), R=[b_g, b_isbt], W=[b_g])
            P.op("dve", lambda e: e.tensor_tensor(out=bb, in0=tmp, in1=cc, op=ALU.add), R=[b_g], W=[b_g])
            P.op("dve", lambda e: e.tensor_tensor(out=G2, in0=li, in1=bb, op=ALU.subtract), R=[b_g], W=[b_g])
            P.op("dve", lambda e: e.tensor_tensor(out=tmp, in0=gbc, in1=G2, op=ALU.add), R=[b_g], W=[b_g])
            P.op("act", lambda e: e.activation(out=dec, in_=bb, func=AF.Exp), R=[b_g], W=[b_g])
            P.op("act", lambda e: e.activation(out=egw, in_=G2, func=AF.Exp), R=[b_g], W=[b_g])
            P.op("act", lambda e: e.activation(out=ww, in_=tmp, func=AF.Exp), R=[b_g], W=[b_g])
            g5 = gbc.rearrange("p n (d q h) -> p n d q h", d=2, q=2)
            P.op("act", lambda e: e.activation(out=EGs[0:64], in_=g5[0:64, :, :, :, 0], func=AF.Exp), R=[b_g], W=[b_g])
            P.op("act", lambda e: e.activation(out=EGs[64:128], in_=g5[64:128, :, :, :, 1], func=AF.Exp), R=[b_g], W=[b_g])
            if MLS < 2:
                return
            for d_ in range(2):
                P.op("dve", lambda e, d_=d_: e.memset(Sst[d_], 0.0), W=[b_S[d_]])
            P.op("dve", lambda e: e.memset(SB, 0.0), W=[b_SB])
            order = [list(range(NT)), [1, 0] + list(range(NT - 1, 1, -1))]
            for step in range(NT):
                for d_ in range(2):
                    n = order[d_][step]
                    S = Sst[d_]
                    sb5 = SB[:, n, d_].rearrange("p (q a) e -> p q a e", a=2)
                    P.op("act", lambda e, S=S, sb5=sb5: e.activation(out=sb5[0:64, :, 0, 0:65], in_=S[0:64], func=AF.Copy), R=[b_S[d_]], W=[b_SB])
                    P.op("act", lambda e, S=S, sb5=sb5: e.activation(out=sb5[64:128, :, 1, 0:65], in_=S[64:128], func=AF.Copy), R=[b_S[d_], b_SB], W=[b_SB])
                    if step == NT - 1:
                        continue
                    P.op("pool", lambda e, n=n, d_=d_: e.tensor_tensor(
                        out=kw[d_], in0=ktok[:, n, :].rearrange("p (h e) -> p h e", e=64),
                        in1=ww[:, n, d_ * 4:(d_ + 1) * 4].unsqueeze(2).broadcast_to([128, 4, 64]), op=ALU.mult),
                        R=[b_ktok, b_g], W=[b_kw[d_]])
                    bk = 2 + d_
                    for h in range(4):
                        P.op("pe", lambda e, h=h, n=n, d_=d_, bk=bk: e.matmul(
                            pb[bk][:, h * 65:(h + 1) * 65], kw[d_].rearrange("p (q a) e -> p q (a e)", a=2)[:, h // 2, :], va[:, n, h, 0:65],
                            start=True, stop=True), R=[b_kw[d_], b_va], W=[bpb[bk]], signal=(h == 3))
                    P.op("dve", lambda e, n=n, d_=d_, S=S: e.tensor_tensor(
                        out=S, in0=S, in1=EGs[:, n, d_, :].unsqueeze(2).broadcast_to([128, 2, 65]), op=ALU.mult),
                        R=[b_S[d_], b_g], W=[b_S[d_]])
                    cl = pb[bk][:, 0:260].rearrange("p (q a e) -> p q a e", a=2, e=65)
                    P.op("dve", lambda e, S=S, cl=cl, d_=d_: e.tensor_tensor(out=S[0:64], in0=S[0:64], in1=cl[0:64, :, 0, :], op=ALU.add),
                         R=[b_S[d_], bpb[bk]], W=[b_S[d_]])
                    P.op("dve", lambda e, S=S, cl=cl, d_=d_: e.tensor_tensor(out=S[64:128], in0=S[64:128], in1=cl[64:128, :, 1, :], op=ALU.add),
                         R=[b_S[d_], bpb[bk]], W=[b_S[d_]])
            if MLS < 3:
                return
            for n in range(NT):
                ts_ = slice(n * 128, (n + 1) * 128)
                P.dma("sp", mot, mo_d[ts_, :], b_mot, R=[b_mo])
                for h in range(4):
                    hs = slice((h % 2) * 64, (h % 2) * 64 + 64)
                    P.op("pe", lambda e, h=h, hs=hs, ts_=ts_: e.matmul(pb[0][:, h * 128:(h + 1) * 128], mks[:, h, ts_], mqs[:, h // 2, ts_],
                                                                        start=True, stop=True), R=[b_mks, b_mqs], W=[bpb[0]], signal=(h == 3))
                kq = pb[0].rearrange("p (h t) -> p h t", t=128)
                P.op("dve", lambda e, kq=kq: e.tensor_tensor(out=STf, in0=kq, in1=mm_[:, 0, :].unsqueeze(1).broadcast_to([128, 4, 128]), op=ALU.mult),
                     R=[bpb[0], b_mm], W=[b_ST])
                P.op("dve", lambda e, kq=kq: e.tensor_tensor(out=STb, in0=kq, in1=mm_[:, 1, :].unsqueeze(1).broadcast_to([128, 4, 128]), op=ALU.mult),
                     R=[bpb[0], b_mm, b_ST], W=[b_ST])
                if MLT < 2:
                    continue
                for d_ in range(2):
                    P.op("pool", lambda e, n=n, d_=d_: e.tensor_tensor(
                        out=vw[:, d_ * 4:(d_ + 1) * 4, :], in0=va[:, n],
                        in1=egw[:, n, d_ * 4:(d_ + 1) * 4].unsqueeze(2).broadcast_to([128, 4, 66]), op=ALU.mult),
                        R=[b_va, b_g, b_vw], W=[b_vw])
                for r in range(8):
                    d_, h = r // 4, r % 4
                    bk = 4 + d_
                    hs = slice((h % 2) * 64, (h % 2) * 64 + 64)
                    ST = STf if d_ == 0 else STb
                    P.op("pe", lambda e, ST=ST, h=h, r=r, bk=bk: e.matmul(pb[bk][:, h * 65:(h + 1) * 65], ST[:, h, :], vw[:, r, 0:65], start=True, stop=False),
                         R=[b_ST, b_vw], W=[bpb[bk]], signal=False)
                    P.op("pe", lambda e, h=h, hs=hs, n=n, d_=d_, bk=bk, ts_=ts_: e.matmul(
                        pb[bk][:, h * 65:(h + 1) * 65], mqs[:, h // 2, ts_], SB[:, n, d_, h, 0:65], start=False, stop=True),
                        R=[b_mqs, b_SB], W=[bpb[bk]], signal=(h == 3))
                if MLT < 3:
                    continue
                for d_ in range(2):
                    bk = 4 + d_
                    P.op("dve", lambda e, d_=d_, bk=bk, n=n: e.tensor_tensor(
                        out=NDs[:, d_ * 4:(d_ + 1) * 4, :], in0=pb[bk][:, 0:260].rearrange("p (h e) -> p h e", e=65),
                        in1=dec[:, n, d_ * 4:(d_ + 1) * 4].unsqueeze(2).broadcast_to([128, 4, 65]), op=ALU.mult),
                        R=[bpb[bk], b_g, b_ND], W=[b_ND])
                if MLS < 4:
                    continue
                P.op("act", lambda e: e.activation(out=dn, in_=NDs[:, :, 64], func=AF.Abs), R=[b_ND], W=[b_dn])
                P.op("dve", lambda e: e.tensor_scalar(out=dn, in0=dn, scalar1=1.0, scalar2=0.0, op0=ALU.max, op1=ALU.add), R=[b_dn], W=[b_dn])
                P.op("dve", lambda e: e.reciprocal(out=dn, in_=dn), R=[b_dn], W=[b_dn])
                P.op("dve", lambda e: e.tensor_tensor(out=Hh, in0=NDs[:, :, 0:64], in1=dn.unsqueeze(2).broadcast_to([128, 8, 64]), op=ALU.mult),
                     R=[b_ND, b_dn], W=[b_Hh])
                P.op("pool", lambda e: e.tensor_tensor(out=Hs, in0=Hh[:, 0:4], in1=Hh[:, 4:8], op=ALU.add), R=[b_Hh], W=[b_Hs])
                P.op("dve", lambda e: e.tensor_reduce(out=st1, in_=Hs, axis=AX.X, op=ALU.add), R=[b_Hs], W=[b_st])
                P.op("dve", lambda e: e.tensor_scalar(out=st1, in0=st1, scalar1=1.0 / 64, scalar2=0.0, op0=ALU.mult, op1=ALU.add), R=[b_st], W=[b_st])
                P.op("dve", lambda e: e.tensor_tensor(out=Hc, in0=Hs, in1=st1.unsqueeze(2).broadcast_to([128, 4, 64]), op=ALU.subtract), R=[b_Hs, b_st], W=[b_Hc])
                P.op("pool", lambda e: e.tensor_tensor(out=Hq, in0=Hc, in1=Hc, op=ALU.mult), R=[b_Hc], W=[b_Hc])
                P.op("dve", lambda e: e.tensor_reduce(out=st2, in_=Hq, axis=AX.X, op=ALU.add), R=[b_Hc, b_st], W=[b_st])
                P.op("act", lambda e: e.activation(out=st2, in_=st2, func=AF.Sqrt, bias=epsc[:, 0:1], scale=1.0 / 64), R=[b_st, b_epsc], W=[b_st])
                P.op("dve", lambda e: e.reciprocal(out=st2, in_=st2), R=[b_st], W=[b_st])
                P.op("dve", lambda e: e.tensor_tensor(out=Hc, in0=Hc, in1=st2.unsqueeze(2).broadcast_to([128, 4, 64]), op=ALU.mult), R=[b_Hc, b_st], W=[b_Hc])
                hc2 = Hc.rearrange("p h e -> p (h e)")
                P.op("pool", lambda e: e.tensor_tensor(out=hc2, in0=hc2, in1=mnb, op=ALU.mult), R=[b_Hc, b_mnb], W=[b_Hc])
                P.op("dve", lambda e: e.tensor_tensor(out=memb, in0=hc2, in1=mot, op=ALU.mult), R=[b_Hc, b_mot], W=[b_memb])
                if MLS < 5:
                    continue
                i2 = n % 2
                pbf = pb[6].bitcast(BF16)[:, 0:256]
                for c in range(2):
                    P.op("pe", lambda e, c=c, pbf=pbf: e.transpose(out=pbf[:, c * 128:(c + 1) * 128], in_=memb[:, c * 128:(c + 1) * 128], identity=identb),
                         R=[b_memb, b_identb], W=[bpb[6]], signal=(c == 1))
                P.op("act", lambda e, i2=i2, pbf=pbf: e.activation(out=memT[i2], in_=pbf.rearrange("p (c t) -> p c t", t=128), func=AF.Copy),
                     R=[bpb[6]], W=[b_memT[i2]])
                P.dma("sp", mixT_d[4:6, :, ts_].rearrange("c p t -> p c t"), memT[i2], b_mix, R=[b_memT[i2]])

        for l in range(L):
            need_ctx = True
            P.barrier()
            A.off = PERS
            wa = [A.alloc([128, 8, 512]) for _ in range(2)]; b_wa = [P.buf("wa%d" % i) for i in range(2)]
            modrow = A.alloc([2, 6 * D]); b_modrow = P.buf("modrow")
            badar = A.alloc([2, 6 * D]); b_badar = P.buf("badar")
            i2t = A.alloc([2, 2]); b_i2t = P.buf("i2t")
            P.dma("sp", badar, bada_d[l:l + 1, :].partition_broadcast(2), b_badar)
            P.dma("sp", i2t, identf_d[0:2, 0:2], b_i2t)
            P.dma("sp", ngt, ng_d[l], b_ngt)
            for ch in range(12):
                i2 = ch % 2
                P.dma("sp", wa[i2], wada_d[l, :, ch * 512:(ch + 1) * 512].rearrange("(k p) n -> p k n", p=128), b_wa[i2])
                bk = ch % 2
                mmg(pb[bk][0:2, :], [(S2[:, kc, :], wa[i2][:, kc, :]) for kc in range(8)], R=[b_S2, b_wa[i2]], W=[bpb[bk]])
                P.op("dve", lambda e, bk=bk, ch=ch: e.tensor_tensor(out=modrow[:, ch * 512:(ch + 1) * 512], in0=pb[bk][0:2, :],
                                                                    in1=badar[:, ch * 512:(ch + 1) * 512], op=ALU.add),
                     R=[bpb[bk], b_badar], W=[b_modrow])
            for c48 in range(48):
                P.op("pe", lambda e, c48=c48: e.matmul(pb[2][:, c48 * 2:c48 * 2 + 2], modrow[:, c48 * 128:(c48 + 1) * 128], i2t,
                                                        start=True, stop=True),
                     R=[b_modrow, b_i2t], W=[bpb[2]], signal=(c48 == 47))
            P.op("dve", lambda e: e.tensor_copy(out=modT.rearrange("p m k w -> p (m k w)"), in_=pb[2][:, 0:96]), R=[bpb[2]], W=[b_modT])
            P.op("dve", lambda e: e.scalar_tensor_tensor(out=gm1, in0=modT[:, 1], scalar=1.0, in1=ngt[:, 0, :].unsqueeze(2).broadcast_to([128, 8, 2]),
                                                         op0=ALU.add, op1=ALU.mult), R=[b_modT, b_ngt], W=[b_gm])
            P.op("dve", lambda e: e.scalar_tensor_tensor(out=gm2, in0=modT[:, 4], scalar=1.0, in1=ngt[:, 2, :].unsqueeze(2).broadcast_to([128, 8, 2]),
                                                         op0=ALU.add, op1=ALU.mult), R=[b_modT, b_ngt, b_gm], W=[b_gm])
            P.op("dve", lambda e: e.tensor_tensor(out=gg1, in0=modT[:, 2], in1=ngt[:, 1, :].unsqueeze(2).broadcast_to([128, 8, 2]), op=ALU.mult),
                 R=[b_modT, b_ngt, b_gm], W=[b_gm])
            P.op("dve", lambda e: e.tensor_tensor(out=gg2, in0=modT[:, 5], in1=ngt[:, 3, :].unsqueeze(2).broadcast_to([128, 8, 2]), op=ALU.mult),
                 R=[b_modT, b_ngt, b_gm], W=[b_gm])

            def norm_mod(xb, nb, which, gm, shift_m, hT, tiles, bufs):
                sq, tn, rstd, sd = tiles
                b_xb, b_sq, b_tn, b_rstd, b_hT = bufs
                P.op("act", lambda e: e.activation(out=sq[:, :, 0:nb], in_=xb[:, :, 0:nb], func=AF.Square), R=[b_xb], W=[b_sq])
                rms_rstd(sq[:, :, 0:nb], nb, 7, b_sq, b_rstd, rstd[:, 0:nb], sd[:, 0:nb])
                P.op("dve", lambda e: e.tensor_tensor(out=tn[:, :, 0:nb], in0=xb[:, :, 0:nb],
                                                      in1=rstd[:, 0:nb].unsqueeze(1).broadcast_to([128, 8, nb]), op=ALU.mult),
                     R=[b_xb, b_rstd], W=[b_tn])
                for kc in range(8):
                    P.op("pool", lambda e, kc=kc: e.tensor_scalar(out=hT[:, kc, 0:nb], in0=tn[:, kc, 0:nb],
                                                                  scalar1=gm[:, kc, which:which + 1],
                                                                  scalar2=modT[:, shift_m, kc, which:which + 1],
                                                                  op0=ALU.mult, op1=ALU.add),
                         R=[b_tn, b_gm, b_modT], W=[b_hT])

            if "p1" in phases:
                P.barrier()
                A.off = PERS
                win = A.alloc([128, 8, DIN], BF16); b_win = P.buf("win")
                for kc in range(8):
                    P.dma("pool", win[:, kc, :], win_d[l, kc * 128:(kc + 1) * 128, :], b_win)
                pm = A.alloc([128, 128]); b_pm = P.buf("pm")
                P.dma("sp", pm, pm_d, b_pm)
                xb = A.alloc([128, 8, 512]); b_xb = P.buf("xb")
                sq = A.alloc([128, 8, 512], BF16); b_sq = P.buf("sq")
                tn = A.alloc([128, 8, 512]); b_tn = P.buf("tn")
                rstd = A.alloc([128, 512]); sd = A.alloc([128, 512]); b_rstd = P.buf("rstd")
                hT = A.alloc([128, 8, 512], BF16); b_hT = P.buf("hT")
                cosb = A.alloc([128, 512]); sinb = A.alloc([128, 512]); b_cs = P.buf("cs")
                qf = [A.alloc([128, 512]) for _ in range(2)]; b_qf = [P.buf("qf%d" % i) for i in range(2)]
                t1 = [A.alloc([128, 512]) for _ in range(2)]; b_t1 = [P.buf("t1%d" % i) for i in range(2)]
                t2 = [A.alloc([128, 512]) for _ in range(2)]; b_t2 = [P.buf("t2%d" % i) for i in range(2)]
                qblk = A.alloc([128, 4, 512], BF16); b_qblk = P.buf("qblk")
                kblk = A.alloc([128, 512], BF16); b_kblk = P.buf("kblk")
                mqb = A.alloc([128, 2, 512], BF16); b_mqb = P.buf("mqb")
                mkTb = A.alloc([128, 2, 512], BF16); b_mkTb = P.buf("mkTb")
                rxb = A.alloc([128, 2, 512]); b_rxb = P.buf("rxb")
                ryb = A.alloc([128, 2, 512]); b_ryb = P.buf("ryb")
                mkt = A.alloc([128, 4, 256], BF16); b_mkt = P.buf("mkt")
                vaug = A.alloc([128, 4, 4, 66], BF16); b_vaug = P.buf("vaug")
                vt = A.alloc([128, 4, 2, 256], BF16); b_vt = P.buf("vt")
                mot = A.alloc([128, 4, 256]); b_mot = P.buf("mot")
                P.op("pool", lambda e: e.memset(vaug, 1.0), W=[b_vaug])
                P.op("pool", lambda e: e.memset(vt, 1.0), W=[b_vt])
                fbank = [0]

                def nextbank():
                    fbank[0] = (fbank[0] + 1) % 4
                    return fbank[0]
                for (t0, nb, which) in blocks(512):
                    if P1S < 1:
                        break
                    P.dma("sp", xb[:, :, 0:nb], xT_d[:, :, t0:t0 + nb].rearrange("k p t -> p k t"), b_xb, R=[b_xT])
                    P.dma("sp", cosb[:, 0:nb], cos_d[:, t0:t0 + nb], b_cs)
                    P.dma("sp", sinb[:, 0:nb], sin_d[:, t0:t0 + nb], b_cs)
                    norm_mod(xb, nb, which, gm1, 0, hT, (sq, tn, rstd, sd), (b_xb, b_sq, b_tn, b_rstd, b_hT))

                    def fm(c0, m):
                        bk = nextbank()
                        mmg(pb[bk][0:m, 0:nb], [(win[:, kc, c0:c0 + m], hT[:, kc, 0:nb]) for kc in range(8)], R=[b_win, b_hT], W=[bpb[bk]])
                        return bk
                    if P1S < 2:
                        continue
                    for j in range(5):
                        i2 = j % 2
                        bk = fm(j * 128, 128)
                        P.op("act", lambda e, bk=bk, i2=i2, j=j: e.activation(out=qf[i2][:, 0:nb], in_=pb[bk][:, 0:nb], func=AF.Copy,
                                                                            scale=(0.125 if j < 4 else 1.0)),
                             R=[bpb[bk]], W=[b_qf[i2]])
                        P.op("pe", lambda e, i2=i2: e.matmul(pb[6][:, 0:nb], pm, qf[i2][:, 0:nb], start=True, stop=True),
                             R=[b_pm, b_qf[i2]], W=[bpb[6]])
                        P.op("dve", lambda e, i2=i2: e.tensor_tensor(out=t1[i2][:, 0:nb], in0=qf[i2][:, 0:nb], in1=cosb[:, 0:nb], op=ALU.mult),
                             R=[b_qf[i2], b_cs], W=[b_t1[i2]])
                        P.op("dve", lambda e, i2=i2: e.tensor_tensor(out=t2[i2][:, 0:nb], in0=pb[6][:, 0:nb], in1=sinb[:, 0:nb], op=ALU.mult),
                             R=[bpb[6], b_cs], W=[b_t2[i2]])
                        if j < 4:
                            P.op("pool", lambda e, i2=i2, j=j: e.tensor_tensor(out=qblk[:, j, 0:nb], in0=t1[i2][:, 0:nb], in1=t2[i2][:, 0:nb], op=ALU.add),
                                 R=[b_t1[i2], b_t2[i2]], W=[b_qblk])
                        else:
                            P.op("pool", lambda e, i2=i2: e.tensor_tensor(out=kblk[:, 0:nb], in0=t1[i2][:, 0:nb], in1=t2[i2][:, 0:nb], op=ALU.add),
                                 R=[b_t1[i2], b_t2[i2]], W=[b_kblk])
                    P.dma("sp", qT_d[:, :, t0:t0 + nb], qblk[:, :, 0:nb], b_q, R=[b_qblk])
                    P.dma("sp", kT_d[:, t0:t0 + nb], kblk[:, 0:nb], b_k, R=[b_kblk])
                    if P1S < 3:
                        continue
                    for c in range(2):
                        bk = fm(640 + c * 128, 128)
                        P.op("act", lambda e, bk=bk, c=c: e.activation(out=mqb[:, c, 0:nb], in_=pb[bk][:, 0:nb], func=AF.Copy), R=[bpb[bk]], W=[b_mqb])
                    P.dma("sp", mqT_d[:, :, t0:t0 + nb].rearrange("c p t -> p c t"), mqb[:, :, 0:nb], b_mq, R=[b_mqb])
                    for c in range(2):
                        bk = fm(1408 + c * 128, 128)
                        P.op("dve", lambda e, bk=bk, c=c: e.tensor_scalar(out=mkTb[:, c, 0:nb], in0=pb[bk][:, 0:nb], scalar1=0.125, scalar2=0.0, op0=ALU.mult, op1=ALU.add), R=[bpb[bk]], W=[b_mkTb])
                    P.dma("sp", mkT_d[:, :, t0:t0 + nb].rearrange("c p t -> p c t"), mkTb[:, :, 0:nb], b_mkT, R=[b_mkTb])
                    for c in range(2):
                        bk = fm(896 + c * 128, 128)
                        P.op("act", lambda e, bk=bk, c=c: e.activation(out=rxb[:, c, 0:nb], in_=pb[bk][:, 0:nb], func=AF.Copy), R=[bpb[bk]], W=[b_rxb])
                    P.dma("sp", rxT_d[:, :, t0:t0 + nb].rearrange("c p t -> p c t"), rxb[:, :, 0:nb], b_rx, R=[b_rxb])
                    for c in range(2):
                        bk = fm(1152 + c * 128, 128)
                        P.op("dve", lambda e, bk=bk, c=c: e.tensor_copy(out=ryb[:, c, 0:nb], in_=pb[bk][:, 0:nb]), R=[bpb[bk]], W=[b_ryb])
                    P.dma("sp", ryT_d[:, :, t0:t0 + nb].rearrange("c p t -> p c t"), ryb[:, :, 0:nb], b_ry, R=[b_ryb])
                    if P1S < 4:
                        continue
                    ntl = nb // 128
                    for tt in range(ntl):
                        mmg(pb[4], [(hT[:, kc, tt * 128:(tt + 1) * 128], win[:, kc, 1408:1920]) for kc in range(8)], R=[b_win, b_hT], W=[bpb[4]])
                        mmg(pb[5][:, 0:400], [(hT[:, kc, tt * 128:(tt + 1) * 128], win[:, kc, 1920:2320]) for kc in range(8)], R=[b_win, b_hT], W=[bpb[5]])
                        if P1T >= 1:
                            P.op("act", lambda e, tt=tt: e.activation(out=mkt[:, tt, :], in_=pb[4][:, 0:256], func=AF.Copy, scale=0.125), R=[bpb[4]], W=[b_mkt])
                        if P1T >= 2:
                            P.op("act", lambda e, tt=tt: e.activation(out=vaug[:, tt, :, 0:64], in_=pb[4][:, 256:512].rearrange("p (h e) -> p h e", e=64), func=AF.Copy),
                                 R=[bpb[4]], W=[b_vaug])
                        if P1T >= 2:
                            P.op("act", lambda e, tt=tt: e.activation(out=vt[:, tt, :, 0:64], in_=pb[5][:, 0:128].rearrange("p (g e) -> p g e", e=64), func=AF.Copy),
                                 R=[bpb[5]], W=[b_vt])
                        if P1T >= 1:
                            P.op("act", lambda e, tt=tt: e.activation(out=vt[:, tt, :, 192:256], in_=pb[5][:, 0:128].rearrange("p (g e) -> p g e", e=64), func=AF.Copy),
                                 R=[bpb[5]], W=[b_vt])
                        if P1T >= 1:
                            P.op("act", lambda e, tt=tt: e.activation(out=mot[:, tt, :], in_=pb[5][:, 128:384], func=AF.Sigmoid), R=[bpb[5]], W=[b_mot])
                        if P1T >= 2:
                            P.op("dve", lambda e, tt=tt, t0=t0: e.tensor_copy(out=gall[:, t0 // 128 + tt, :], in_=pb[5][:, 384:400]), R=[bpb[5]], W=[b_gall])
                    if P1S < 5:
                        continue
                    tsl = slice(t0, t0 + nb)
                    P.dma("sp", mk_d[tsl, :].rearrange("(n p) c -> p n c", p=128), mkt[:, 0:ntl, :], b_mk, R=[b_mkt])
                    P.dma("sp", mva_d[tsl, :].rearrange("(n p) c -> p n c", p=128), vaug[:, 0:ntl].rearrange("p n h e -> p n (h e)"), b_mva, R=[b_vaug])
                    P.dma("sp", vab_d[tsl, :].rearrange("(n p) c -> p n c", p=128), vt[:, 0:ntl].rearrange("p n g e -> p n (g e)"), b_vab, R=[b_vt])
                    P.dma("sp", mo_d[tsl, :].rearrange("(n p) c -> p n c", p=128), mot[:, 0:ntl, :], b_mo, R=[b_mot])

            if "att" in phases:
                P.barrier()
                A.off = PERS
                qTs = A.alloc([128, 4, T], BF16); b_qTs = P.buf("qTs")
                kTs = A.alloc([128, T], BF16); b_kTs = P.buf("kTs")
                vab = A.alloc([128, NT, 2, 2, 128], BF16); b_vabs = P.buf("vabs")
                am = A.alloc([128, 2, 128], BF16); b_am = P.buf("am")
                esr = A.alloc([1, 2, 512]); esb = A.alloc([1, 2, 512], BF16); b_es = P.buf("es")
                selAB = A.alloc([1, 2, 128], BF16); b_selAB = P.buf("selAB")
                NPT = 6
                PT = [A.alloc([128, 512], BF16) for _ in range(NPT)]; b_PT = [P.buf("PT%d" % i) for i in range(NPT)]
                rden = [A.alloc([128, 512]) for _ in range(2)]; b_rden = [P.buf("rden%d" % i) for i in range(2)]
                mxo = [A.alloc([128, 2, 128], BF16) for _ in range(2)]; b_mxo = [P.buf("mxo%d" % i) for i in range(2)]
                P.dma("sp", qTs, qT_d, b_qTs, R=[b_q])
                P.dma("sp", kTs, kT_d, b_kTs, R=[b_k])
                P.dma("sp", vab.rearrange("p n g a e -> p n (g a e)"), vab_d.rearrange("(n p) c -> p n c", p=128), b_vabs, R=[b_vab])
                P.dma("pool", am, amask_d.rearrange("m p q -> p m q"), b_am)
                P.dma("sp", esr, sink_d[l:l + 1].rearrange("o g n -> o g n"), b_es)
                P.op("act", lambda e: e.activation(out=esb, in_=esr, func=AF.Exp), R=[b_es], W=[b_es])
                P.op("dve", lambda e: e.memset(selAB, 0.0), W=[b_selAB])
                P.op("dve", lambda e: e.memset(selAB[:, 0, 64:128], 1.0), R=[b_selAB], W=[b_selAB])
                P.op("dve", lambda e: e.memset(selAB[:, 1, 0:64], 1.0), R=[b_selAB], W=[b_selAB])
                ptc = 0
                it = 0
                qtiles = list(range(2, NT)) + [0, 1]
                for qt in qtiles:
                    if qt >= 2:
                        n = qt - 2
                        chunks = []
                        if n > 0:
                            chunks.append((qt - 1, 0))
                        chunks.append((qt, None))
                        if n < NLT - 1:
                            chunks.append((qt + 1, 1))
                        chunks += [(0, None), (1, None)]
                    else:
                        chunks = [(0, None), (1, None)]
                    for g in range(2):
                        bA = 4 + 2 * (it % 2)
                        bB = bA + 1
                        gs = slice(g * 64, (g + 1) * 64)
                        first = True
                        for ci, (kt, mk_) in enumerate(chunks):
                            sb = ci % 4
                            P.op("pe", lambda e, sb=sb, kt=kt, gs=gs, qt=qt: e.matmul(
                                pb[sb].rearrange("p (j q) -> p j q", q=128), kTs[gs, kt * 128:(kt + 1) * 128], qTs[gs, :, qt * 128:(qt + 1) * 128],
                                start=True, stop=True), R=[b_kTs, b_qTs], W=[bpb[sb]])
                            pi = ptc % NPT
                            ptc += 1
                            P.op("act", lambda e, sb=sb, pi=pi: e.activation(out=PT[pi], in_=pb[sb], func=AF.Exp), R=[bpb[sb]], W=[b_PT[pi]])
                            if mk_ is not None:
                                P.op("dve", lambda e, pi=pi, mk_=mk_: e.tensor_tensor(
                                    out=PT[pi].rearrange("p (j q) -> p j q", q=128), in0=PT[pi].rearrange("p (j q) -> p j q", q=128),
                                    in1=am[:, mk_, :].unsqueeze(1).broadcast_to([128, 4, 128]), op=ALU.mult), R=[b_PT[pi], b_am], W=[b_PT[pi]])
                            P.op("pe", lambda e, bA=bA, kt=kt, g=g, pi=pi, first=first: e.matmul(pb[bA], vab[:, kt, g, 0, :], PT[pi], start=first, stop=False),
                                 R=[b_vabs, b_PT[pi]], W=[bpb[bA]], signal=False)
                            P.op("pe", lambda e, bB=bB, kt=kt, g=g, pi=pi, first=first: e.matmul(pb[bB], vab[:, kt, g, 1, :], PT[pi], start=first, stop=False),
                                 R=[b_vabs, b_PT[pi]], W=[bpb[bB]], signal=True)
                            first = False
                        P.op("pe", lambda e, bA=bA, g=g: e.matmul(pb[bA], selAB[:, 0, :], esb[:, g, :], start=False, stop=True),
                             R=[b_selAB, b_es], W=[bpb[bA]], signal=False)
                        P.op("pe", lambda e, bB=bB, g=g: e.matmul(pb[bB], selAB[:, 1, :], esb[:, g, :], start=False, stop=True),
                             R=[b_selAB, b_es], W=[bpb[bB]], signal=True)
                        r2 = it % 2
                        P.op("dve", lambda e, bB=bB, r2=r2: e.reciprocal(out=rden[r2][0:64, :], in_=pb[bB][0:64, :]), R=[bpb[bB]], W=[b_rden[r2]])
                        P.op("dve", lambda e, bA=bA, r2=r2: e.reciprocal(out=rden[r2][64:128, :], in_=pb[bA][64:128, :]), R=[bpb[bA]], W=[b_rden[r2]])
                        P.op("dve", lambda e, bA=bA, r2=r2: e.tensor_tensor(
                            out=mxo[r2][0:64], in0=pb[bA][0:64, :].rearrange("p (a b q) -> p a b q", b=2, q=128)[:, :, 0, :],
                            in1=rden[r2][0:64, :].rearrange("p (a b q) -> p a b q", b=2, q=128)[:, :, 0, :], op=ALU.mult),
                            R=[bpb[bA], b_rden[r2]], W=[b_mxo[r2]])
                        P.op("dve", lambda e, bB=bB, r2=r2: e.tensor_tensor(
                            out=mxo[r2][64:128], in0=pb[bB][64:128, :].rearrange("p (a b q) -> p a b q", b=2, q=128)[:, :, 1, :],
                            in1=rden[r2][64:128, :].rearrange("p (a b q) -> p a b q", b=2, q=128)[:, :, 1, :], op=ALU.mult),
                            R=[bpb[bB], b_rden[r2], b_mxo[r2]], W=[b_mxo[r2]])
                        P.dma("sp", mixT_d[2 * g:2 * g + 2, :, qt * 128:(qt + 1) * 128].rearrange("c p t -> p c t"), mxo[r2], b_mix, R=[b_mxo[r2]])
                        it += 1

            if "lru" in phases:
                P.barrier()
                A.off = PERS
                TP = T + 6
                xp = A.alloc([128, TP]); b_xp = P.buf("xp")
                seq = A.alloc([128, T]); b_seq = P.buf("seq")
                hacc = A.alloc([128, T]); b_hacc = P.buf("hacc")
                yt = A.alloc([128, T]); b_yt = P.buf("yt")
                ga = A.alloc([128, T]); b_ga = P.buf("ga")
                gi = A.alloc([128, T]); b_gi = P.buf("gi")
                gt = A.alloc([128, T]); b_gt = P.buf("gt")
                hd = A.alloc([128, T]); b_hd = P.buf("hd")
                recb = A.alloc([128, T], BF16); b_recb = P.buf("recb")
                bdt = A.alloc([128, 8, 128]); b_bdt = P.buf("bdt")
                cwt = A.alloc([128, 2, 4]); cbt = A.alloc([128, 2]); gbt = A.alloc([128, 8]); lamt = A.alloc([128, 4])
                cdt = A.alloc([128, 4]); b_lp = P.buf("lrup")
                P.dma("sp", bdt, bd_d[l].rearrange("m p q -> p m q"), b_bdt)
                P.dma("sp", cwt, cw_d[l], b_lp)
                P.dma("sp", cbt, cb_d[l], b_lp)
                P.dma("sp", gbt, gb_d[l], b_lp)
                P.dma("sp", lamt, lam_d[l], b_lp)
                P.op("act", lambda e: e.activation(out=cdt, in_=lamt, func=AF.Exp, scale=-1.0), R=[b_lp], W=[b_lp])
                P.op("act", lambda e: e.activation(out=cdt, in_=cdt, func=AF.Ln, bias=1.0, scale=1.0), R=[b_lp], W=[b_lp])
                P.op("dve", lambda e: e.tensor_scalar(out=cdt, in0=cdt, scalar1=-8.0, scalar2=0.0, op0=ALU.mult, op1=ALU.add), R=[b_lp], W=[b_lp])
                streams = [(0, NCTX, 2), (NCTX, T, 5)]
                for ct in range(2):
                    P.op("dve", lambda e: e.memset(xp, 0.0), W=[b_xp])
                    P.dma("sp", xp[:, 2:2 + NCTX], rxT_d[ct, :, 0:NCTX], b_xp, R=[b_rx])
                    P.dma("sp", xp[:, 5 + NCTX:5 + T], rxT_d[ct, :, NCTX:T], b_xp, R=[b_rx])
                    P.dma("sp", yt, ryT_d[ct], b_yt, R=[b_ry])
                    for (lo, hi, off) in streams:
                        P.op("dve", lambda e, lo=lo, hi=hi, off=off: e.tensor_scalar(
                            out=seq[:, lo:hi], in0=xp[:, lo + off - 2:hi + off - 2], scalar1=cwt[:, ct, 0:1], scalar2=cbt[:, ct:ct + 1],
                            op0=ALU.mult, op1=ALU.add), R=[b_xp, b_lp], W=[b_seq])
                        for j in range(1, 4):
                            P.op("dve", lambda e, lo=lo, hi=hi, off=off, j=j: e.scalar_tensor_tensor(
                                out=seq[:, lo:hi], in0=xp[:, lo + off - 2 + j:hi + off - 2 + j], scalar=cwt[:, ct, j:j + 1], in1=seq[:, lo:hi],
                                op0=ALU.mult, op1=ALU.add), R=[b_xp, b_lp, b_seq], W=[b_seq])
                    for d in range(2):
                        for gate, dst, bdst in ((0, ga, b_ga), (1, gi, b_gi)):
                            m = (d * 2 + gate) * 2 + ct
                            for bi_, t0 in enumerate(range(0, T, 512)):
                                nb = min(512, T - t0)
                                bk = bi_ % 4
                                P.op("pe", lambda e, bk=bk, m=m, t0=t0, nb=nb: e.matmul(pb[bk][:, 0:nb], bdt[:, m, :], seq[:, t0:t0 + nb], start=True, stop=True),
                                     R=[b_bdt, b_seq], W=[bpb[bk]])
                                P.op("act", lambda e, bk=bk, m=m, t0=t0, nb=nb, dst=dst: e.activation(
                                    out=dst[:, t0:t0 + nb], in_=pb[bk][:, 0:nb], func=AF.Sigmoid, bias=gbt[:, m:m + 1], scale=1.0),
                                    R=[bpb[bk], b_lp], W=[bdst])
                        ci = d * 2 + ct
                        P.op("act", lambda e, ci=ci: e.activation(out=ga, in_=ga, func=AF.Exp, scale=cdt[:, ci:ci + 1]), R=[b_ga, b_lp], W=[b_ga])
                        P.op("dve", lambda e: e.tensor_tensor(out=gt, in0=ga, in1=ga, op=ALU.mult), R=[b_ga], W=[b_gt])
                        P.op("act", lambda e: e.activation(out=gt, in_=gt, func=AF.Sqrt, bias=1.0, scale=-1.0), R=[b_gt], W=[b_gt])
                        P.op("dve", lambda e: e.tensor_tensor(out=gi, in0=gi, in1=gt, op=ALU.mult), R=[b_gi, b_gt], W=[b_gi])
                        P.op("pool", lambda e: e.tensor_tensor(out=gi, in0=gi, in1=seq, op=ALU.mult), R=[b_gi, b_seq], W=[b_gi])
                        if d == 0:
                            P.op("dve", lambda e: e.tensor_tensor_scan(out=hacc, data0=ga, data1=gi, initial=0.0, op0=ALU.mult, op1=ALU.add),
                                 R=[b_ga, b_gi], W=[b_hacc])
                        else:
                            P.op("dve", lambda e: e.tensor_tensor_scan(out=hd[:, NCTX - 1::-1], data0=ga[:, NCTX - 1::-1], data1=gi[:, NCTX - 1::-1],
                                                                       initial=0.0, op0=ALU.mult, op1=ALU.add), R=[b_ga, b_gi], W=[b_hd])
                            P.op("dve", lambda e: e.tensor_tensor_scan(out=hd[:, T - 1:NCTX - 1:-1], data0=ga[:, T - 1:NCTX - 1:-1],
                                                                       data1=gi[:, T - 1:NCTX - 1:-1], initial=hd[:, 0:1],
                                                                       op0=ALU.mult, op1=ALU.add), R=[b_ga, b_gi, b_hd], W=[b_hd])
                            P.op("pool", lambda e: e.tensor_tensor(out=hacc, in0=hacc, in1=hd, op=ALU.add), R=[b_hacc, b_hd], W=[b_hacc])
                    P.op("dve", lambda e: e.tensor_tensor(out=gt, in0=yt, in1=yt, op=ALU.mult), R=[b_yt], W=[b_gt])
                    P.op("dve", lambda e: e.tensor_scalar(out=gt, in0=gt, scalar1=0.044715, scalar2=1.0, op0=ALU.mult, op1=ALU.add), R=[b_gt], W=[b_gt])
                    P.op("dve", lambda e: e.tensor_tensor(out=gt, in0=gt, in1=yt, op=ALU.mult), R=[b_gt, b_yt], W=[b_gt])
                    P.op("act", lambda e: e.activation(out=gt, in_=gt, func=AF.Sigmoid, scale=1.5957691216057308), R=[b_gt], W=[b_gt])
                    P.op("dve", lambda e: e.tensor_tensor(out=gt, in0=gt, in1=yt, op=ALU.mult), R=[b_gt, b_yt], W=[b_gt])
                    P.op("dve", lambda e: e.tensor_tensor(out=recb, in0=gt, in1=hacc, op=ALU.mult), R=[b_gt, b_hacc], W=[b_recb])
                    P.dma("sp", mixT_d[6 + ct], recb, b_mix, R=[b_recb])

            if "mls" in phases:
                P.barrier()
                A.off = PERS
                mlstm_phase(l)

            if "p3a" in phases:
                P.barrier()
                A.off = PERS
                wo = A.alloc([128, 8, D], BF16); b_wo = P.buf("wo")
                for kc in range(8):
                    P.dma("pool", wo[:, kc, :], wout_d[l, kc * 128:(kc + 1) * 128, :], b_wo)
                xb = A.alloc([128, 8, 512]); b_xb = P.buf("xb3")
                mxb = A.alloc([128, 8, 512], BF16); b_mxb = P.buf("mxb")
                ysb = A.alloc([128, 8, 512]); b_ysb = P.buf("ysb")
                sq = A.alloc([128, 8, 512], BF16); b_sq = P.buf("sq3")
                tn = A.alloc([128, 8, 512]); b_tn = P.buf("tn3")
                rstd = A.alloc([128, 512]); sd = A.alloc([128, 512]); b_rstd = P.buf("rstd3")
                hT = A.alloc([128, 8, 512], BF16); b_hT = P.buf("h2b")
                for (t0, nb, which) in blocks(512):
                    P.dma("sp", xb[:, :, 0:nb], xT_d[:, :, t0:t0 + nb].rearrange("k p t -> p k t"), b_xb, R=[b_xT])
                    P.dma("sp", mxb[:, :, 0:nb], mixT_d[:, :, t0:t0 + nb].rearrange("k p t -> p k t"), b_mxb, R=[b_mix])
                    for dc in range(8):
                        bk = dc % 4
                        mmg(pb[bk][:, 0:nb], [(wo[:, kc, dc * 128:(dc + 1) * 128], mxb[:, kc, 0:nb]) for kc in range(8)], R=[b_wo, b_mxb], W=[bpb[bk]])
                        if dc % 2 == 0:
                            P.op("act", lambda e, bk=bk, dc=dc: e.activation(out=ysb[:, dc, 0:nb], in_=pb[bk][:, 0:nb], func=AF.Copy), R=[bpb[bk]], W=[b_ysb])
                        else:
                            P.op("dve", lambda e, bk=bk, dc=dc: e.tensor_copy(out=ysb[:, dc, 0:nb], in_=pb[bk][:, 0:nb]), R=[bpb[bk]], W=[b_ysb])
                    P.op("act", lambda e: e.activation(out=sq[:, :, 0:nb], in_=ysb[:, :, 0:nb], func=AF.Square), R=[b_ysb], W=[b_sq])
                    rms_rstd(sq[:, :, 0:nb], nb, 6, b_sq, b_rstd, rstd[:, 0:nb], sd[:, 0:nb])
                    P.op("dve", lambda e: e.tensor_tensor(out=tn[:, :, 0:nb], in0=ysb[:, :, 0:nb],
                                                          in1=rstd[:, 0:nb].unsqueeze(1).broadcast_to([128, 8, nb]), op=ALU.mult),
                         R=[b_ysb, b_rstd], W=[b_tn])
                    for kc in range(8):
                        P.op("dve", lambda e, kc=kc: e.scalar_tensor_tensor(
                            out=xb[:, kc, 0:nb], in0=tn[:, kc, 0:nb], scalar=gg1[:, kc, which:which + 1], in1=xb[:, kc, 0:nb],
                            op0=ALU.mult, op1=ALU.add), R=[b_tn, b_gm, b_xb], W=[b_xb])
                    P.dma("sp", xT_d[:, :, t0:t0 + nb].rearrange("k p t -> p k t"), xb[:, :, 0:nb], b_xT, R=[b_xb])
                    norm_mod(xb, nb, which, gm2, 3, hT, (sq, tn, rstd, sd), (b_xb, b_sq, b_tn, b_rstd, b_hT))
                    P.dma("sp", h2T_d[:, :, t0:t0 + nb].rearrange("k p t -> p k t"), hT[:, :, 0:nb], b_h2, R=[b_hT])

            if "p3b" in phases:
                P.barrier()
                A.off = PERS
                wfi = A.alloc([128, 8, 2 * DFF], BF16); b_wfi = P.buf("wfi")
                wfo = A.alloc([128, 22, D], BF16); b_wfo = P.buf("wfo")
                for kc in range(8):
                    for hh_ in range(4):
                        cs_ = slice(hh_ * 1408, (hh_ + 1) * 1408)
                        P.dma("pool", wfi[:, kc, cs_], wfi_d[l, kc * 128:(kc + 1) * 128, cs_], b_wfi)
                for f in range(22):
                    P.dma("pool", wfo[:, f, :], wfo_d[l, f * 128:(f + 1) * 128, :], b_wfo)
                NB = 256
                xb = A.alloc([128, 8, NB]); b_xb = P.buf("xb4")
                h2 = A.alloc([128, 8, NB], BF16); b_h2b = P.buf("h2b4")
                actT = A.alloc([128, 22, NB], BF16); b_actT = P.buf("actT")
                sg = [A.alloc([128, NB]) for _ in range(2)]; b_sg = [P.buf("sg%d" % i) for i in range(2)]
                ysb = A.alloc([128, 8, NB]); b_ysb = P.buf("ysb4")
                sq = A.alloc([128, 8, NB], BF16); b_sq = P.buf("sq4")
                rstd = A.alloc([128, NB]); sd = A.alloc([128, NB]); b_rstd = P.buf("rstd4")
                tn4 = A.alloc([128, 8, NB]); b_tn4 = P.buf("tn4")
                for (t0, nb, which) in blocks(NB):
                    P.dma("sp", xb, xT_d[:, :, t0:t0 + nb].rearrange("k p t -> p k t"), b_xb, R=[b_xT])
                    P.dma("sp", h2, h2T_d[:, :, t0:t0 + nb].rearrange("k p t -> p k t"), b_h2b, R=[b_h2])
                    for f in range(22):
                        bg = (2 * f) % 4
                        bu = bg + 1
                        mmg(pb[bg][:, 0:nb], [(wfi[:, kc, f * 128:(f + 1) * 128], h2[:, kc, :]) for kc in range(8)], R=[b_wfi, b_h2b], W=[bpb[bg]])
                        mmg(pb[bu][:, 0:nb], [(wfi[:, kc, DFF + f * 128:DFF + (f + 1) * 128], h2[:, kc, :]) for kc in range(8)], R=[b_wfi, b_h2b], W=[bpb[bu]])
                        s2 = f % 2
                        P.op("act", lambda e, bg=bg, s2=s2: e.activation(out=sg[s2], in_=pb[bg][:, 0:nb], func=AF.Silu), R=[bpb[bg]], W=[b_sg[s2]])
                        P.op("dve", lambda e, bu=bu, s2=s2, f=f: e.tensor_tensor(out=actT[:, f, :], in0=pb[bu][:, 0:nb], in1=sg[s2], op=ALU.mult),
                             R=[bpb[bu], b_sg[s2]], W=[b_actT])
                    for dc in range(8):
                        bk = 4 + dc % 3
                        mmg(pb[bk][:, 0:nb], [(wfo[:, f, dc * 128:(dc + 1) * 128], actT[:, f, :]) for f in range(22)], R=[b_wfo, b_actT], W=[bpb[bk]])
                        if dc % 2 == 0:
                            P.op("act", lambda e, bk=bk, dc=dc: e.activation(out=ysb[:, dc, :], in_=pb[bk][:, 0:nb], func=AF.Copy), R=[bpb[bk]], W=[b_ysb])
                        else:
                            P.op("dve", lambda e, bk=bk, dc=dc: e.tensor_copy(out=ysb[:, dc, :], in_=pb[bk][:, 0:nb]), R=[bpb[bk]], W=[b_ysb])
                    P.op("act", lambda e: e.activation(out=sq, in_=ysb, func=AF.Square), R=[b_ysb], W=[b_sq])
                    rms_rstd(sq, nb, 7, b_sq, b_rstd, rstd, sd)
                    P.op("dve", lambda e: e.tensor_tensor(out=tn4, in0=ysb, in1=rstd.unsqueeze(1).broadcast_to([128, 8, NB]), op=ALU.mult),
                         R=[b_ysb, b_rstd], W=[b_tn4])
                    for kc in range(8):
                        P.op("dve", lambda e, kc=kc: e.scalar_tensor_tensor(
                            out=xb[:, kc, :], in0=tn4[:, kc, :], scalar=gg2[:, kc, which:which + 1], in1=xb[:, kc, :],
                            op0=ALU.mult, op1=ALU.add), R=[b_tn4, b_gm, b_xb], W=[b_xb])
                    P.dma("sp", xT_d[:, :, t0:t0 + nb].rearrange("k p t -> p k t"), xb, b_xT, R=[b_xb])

        P.barrier()
        A.off = PERS
        xo = [A.alloc([128, 8, 128]) for _ in range(2)]; b_xo = [P.buf("xo%d" % i) for i in range(2)]
        xtk = [A.alloc([128, D]) for _ in range(2)]; b_xtk = [P.buf("xtk%d" % i) for i in range(2)]
        for tt in range(NT):
            if tt < 2 and not ctx_out:
                continue
            i2 = tt % 2
            P.dma("sp", xo[i2], xT_d[:, :, tt * 128:(tt + 1) * 128].rearrange("k p t -> p k t"), b_xo[i2], R=[b_xT])
            for half in range(2):
                bk = (tt * 2 + half) % 4
                for c4 in range(4):
                    kc = half * 4 + c4
                    P.op("pe", lambda e, kc=kc, c4=c4, bk=bk, i2=i2: e.transpose(
                        out=pb[bk][:, c4 * 128:(c4 + 1) * 128], in_=xo[i2][:, kc, :], identity=identf),
                        R=[b_xo[i2], b_identf], W=[bpb[bk]], signal=(c4 == 3))
                if half == 0:
                    P.op("act", lambda e, bk=bk, i2=i2: e.activation(out=xtk[i2][:, 0:512], in_=pb[bk], func=AF.Copy), R=[bpb[bk]], W=[b_xtk[i2]])
                else:
                    P.op("dve", lambda e, bk=bk, i2=i2: e.tensor_copy(out=xtk[i2][:, 512:1024], in_=pb[bk]), R=[bpb[bk]], W=[b_xtk[i2]])
            dst = ctxo_d[tt * 128:(tt + 1) * 128, :] if tt < 2 else out_d[(tt - 2) * 128:(tt - 1) * 128, :]
            P.dma("sp", dst, xtk[i2], b_out, R=[b_xtk[i2]])
        P._waits("sp", [b_out], [])
        P.emit()
        print("instructions:", P.nins, "sems:", P.nsem + 4, {k: P.n[k] for k in P.n})
    return nc


def _win_perm():
    q = [(g * 4 + j) * 64 + d for j in range(4) for g in range(2) for d in range(64)]
    k = list(range(512, 640))
    v = list(range(640, 768))
    mq = list(range(768, 1024))
    mk = list(range(1024, 1280))
    mv = list(range(1280, 1536))
    mo = list(range(1536, 1792))
    mg = [1792 + i for i in (0, 1, 2, 3, 8, 9, 10, 11, 4, 5, 6, 7, 12, 13, 14, 15)]
    rx = list(range(1808, 2064))
    ry = list(range(2064, 2320))
    return np.array(q + k + mq + rx + ry + mk + mv + v + mo + mg)


def _consts(TL):
    T = NCTX + TL
    c = {}
    c["ident"] = np.eye(128, dtype=np.float32)
    pm = np.zeros((128, 128), np.float32)
    for p in range(128):
        d = p % 64
        src = p + 16 if (d % 32) < 16 else p - 16
        pm[src, p] = 1.0
    c["pm"] = pm
    freqs = (10000.0 ** (-np.arange(16, dtype=np.float32) / 16)).astype(np.float32)
    t = np.arange(TL)
    row = (t // 64).astype(np.float32)
    col = (t % 64).astype(np.float32)
    ang_r = (row[:, None] * freqs[None, :]).astype(np.float32)
    ang_c = (col[:, None] * freqs[None, :]).astype(np.float32)
    cosT = np.ones((128, T), np.float32)
    sinT = np.zeros((128, T), np.float32)
    for p in range(128):
        d = p % 64
        ang = ang_r if d < 32 else ang_c
        f = d % 16
        sgn = -1.0 if (d % 32) < 16 else 1.0
        cosT[p, NCTX:] = np.cos(ang[:, f])
        sinT[p, NCTX:] = sgn * np.sin(ang[:, f])
    c["cosT"] = cosT
    c["sinT"] = sinT
    j = np.arange(128)[:, None]
    i = np.arange(128)[None, :]
    c["amask"] = np.stack([(j >= i), (j <= i)]).astype(np.float32)
    c["mmask"] = np.stack([(j <= i), (j >= i)]).astype(np.float32)
    return c


def _prep_weights(inp, L):
    w = {}
    perm = _win_perm()
    w["w_ada"] = np.ascontiguousarray(inp["w_ada"][:L])
    w["b_ada"] = np.ascontiguousarray(inp["b_ada"][:L])
    w["ng"] = np.ascontiguousarray(inp["norm_gain"][:L].reshape(L, 4, 8, 128).transpose(0, 3, 1, 2))
    w["w_in"] = np.ascontiguousarray(inp["w_in"][:L][:, :, perm])
    w["w_out"] = np.ascontiguousarray(inp["w_out"][:L])
    w["w_ffn_in"] = np.ascontiguousarray(inp["w_ffn_in"][:L])
    w["w_ffn_out"] = np.ascontiguousarray(inp["w_ffn_out"][:L])
    sk = inp["attn_sink"][:L].reshape(L, 2, 4)
    w["sink_rep"] = np.ascontiguousarray(np.repeat(sk, 128, axis=2))
    gbm = inp["mlstm_gate_b"][:L]
    w["mbi"] = np.ascontiguousarray(np.concatenate([gbm[:, 0], gbm[:, 2]], axis=1))
    w["mbf"] = np.ascontiguousarray(np.concatenate([gbm[:, 1], gbm[:, 3]], axis=1))
    w["mnorm"] = np.ascontiguousarray(inp["mlstm_norm"][:L])
    w["cw"] = np.ascontiguousarray(inp["conv_w"][:L].reshape(L, 4, 2, 128).transpose(0, 3, 2, 1))
    w["cb"] = np.ascontiguousarray(inp["conv_b"][:L].reshape(L, 2, 128).transpose(0, 2, 1))
    gw = inp["lru_gate_w"][:L]
    bd = np.zeros((L, 8, 128, 128), np.float32)
    for d in range(2):
        for gate in range(2):
            for ct in range(2):
                m = (d * 2 + gate) * 2 + ct
                bd[:, m, 0:64, 0:64] = gw[:, d, gate, 2 * ct]
                bd[:, m, 64:128, 64:128] = gw[:, d, gate, 2 * ct + 1]
    w["bd"] = bd
    w["gb"] = np.ascontiguousarray(inp["lru_gate_b"][:L].reshape(L, 2, 2, 2, 128).transpose(0, 4, 1, 2, 3).reshape(L, 128, 8))
    w["lam"] = np.ascontiguousarray(inp["lru_lam"][:L].reshape(L, 2, 2, 128).transpose(0, 3, 1, 2).reshape(L, 128, 4))
    return w


def _core_inputs(inp, b, x_b, ctx_b):
    cc = np.stack([inp["c"][b], inp["c_ctx"]], axis=1)
    cT = np.ascontiguousarray(cc.reshape(8, 128, 2).transpose(1, 0, 2))
    return {"x": np.ascontiguousarray(x_b), "ctx": np.ascontiguousarray(ctx_b), "cT": cT}


_CACHE = {}


def kernel(**inputs):
    inp = {k: np.asarray(v) for k, v in inputs.items()}
    B, TL, _ = inp["x"].shape
    L = inp["w_ada"].shape[0]
    key = (L, TL)
    if key not in _CACHE:
        _CACHE[key] = build(L, TL)
    nc = _CACHE[key]
    w = _prep_weights(inp, L)
    c = _consts(TL)
    in_maps = []
    for b in range(B):
        m = dict(w)
        m.update(c)
        m.update(_core_inputs(inp, b, inp["x"][b], inp["ctx"][b]))
        in_maps.append(m)
    res = run_bass_kernel_spmd(nc, in_maps, core_ids=list(range(B)))
    out = np.stack([np.asarray(r["out"]) for r in res.results], axis=0)
    return out.astype(np.float32)
```
